# Optimizing a Trainium2 kernel written in Bass

```python
import math
import jax, jax.numpy as jnp
from jax import lax
import numpy as np

D_MODEL = 1024
BATCH = 8
SEQ = 8192
DEPTH = 2

D_MIX = D_MODEL
S5_WIDTH = D_MIX // 4
S5_GROUP = 16
S5_GROUPS = S5_WIDTH // S5_GROUP
S5_STATE = 64
S5_MIN_DECAY = 1e-4
GLA_WIDTH = D_MIX // 2
GLA_HEADS = 4
GLA_DV = GLA_WIDTH // GLA_HEADS
GLA_DK = GLA_DV // 2
GLA_KEY_WIDTH = GLA_HEADS * GLA_DK
GLA_GATE_RANK = 16
GLA_TAU = 16.0
GLA_CHUNK = 64
LRU_WIDTH = D_MIX - S5_WIDTH - GLA_WIDTH
LRU_BLOCKS = 8
LRU_BLOCK = LRU_WIDTH // LRU_BLOCKS
LRU_CONV = 4
LRU_C = 8.0
D_FF = 4 * D_MODEL
DEEPNORM_ALPHA = (2 * DEPTH) ** 0.25
DEEPNORM_BETA = (8 * DEPTH) ** -0.25
LN_EPS = 1e-5
RMS_EPS = 1e-6
SPLIT_SIZES = (S5_WIDTH, GLA_KEY_WIDTH, GLA_KEY_WIDTH, GLA_WIDTH, GLA_WIDTH,
               GLA_GATE_RANK, LRU_WIDTH, LRU_WIDTH)
D_IN = sum(SPLIT_SIZES)

kernel_name = 'hymba_style_s5_gla_rglru_deepnorm'


def _layer_norm(x, g, b):
    xf = x.astype(jnp.float32)
    mu = jnp.mean(xf, axis=-1, keepdims=True)
    var = jnp.mean(jnp.square(xf - mu), axis=-1, keepdims=True)
    return ((xf - mu) * lax.rsqrt(var + LN_EPS) * g + b).astype(x.dtype)


def _complex_affine_combine(e1, e2):
    ar1, ai1, br1, bi1 = e1
    ar2, ai2, br2, bi2 = e2
    ar = ar2 * ar1 - ai2 * ai1
    ai = ar2 * ai1 + ai2 * ar1
    br = ar2 * br1 - ai2 * bi1 + br2
    bi = ar2 * bi1 + ai2 * br1 + bi2
    return (ar, ai, br, bi)


def _real_affine_combine(e1, e2):
    a1, b1 = e1
    a2, b2 = e2
    return (a2 * a1, a2 * b1 + b2)


def _s5_mixer(u, lam_re, lam_im, log_dt, b_re, b_im, c_re, c_im, d_skip, w_glu):
    bsz, seq, _ = u.shape
    f32 = jnp.float32
    u = u.astype(f32)
    ug = u.reshape(bsz, seq, S5_GROUPS, S5_GROUP)
    dt = jnp.exp(log_dt.astype(f32))[:, None]
    lr = jnp.minimum(lam_re.astype(f32), -S5_MIN_DECAY)
    li = lam_im.astype(f32)
    mag = jnp.exp(lr * dt)
    a_re = mag * jnp.cos(li * dt)
    a_im = mag * jnp.sin(li * dt)
    den = lr * lr + li * li
    f_re = ((a_re - 1.0) * lr + a_im * li) / den
    f_im = (a_im * lr - (a_re - 1.0) * li) / den
    bb_re = f_re[..., None] * b_re - f_im[..., None] * b_im
    bb_im = f_re[..., None] * b_im + f_im[..., None] * b_re
    bu_re = jnp.einsum('blgc,gpc->blgp', ug, bb_re)
    bu_im = jnp.einsum('blgc,gpc->blgp', ug, bb_im)
    shape_a = (1, seq, S5_GROUPS, S5_STATE)
    a_re_t = jnp.broadcast_to(a_re, shape_a)
    a_im_t = jnp.broadcast_to(a_im, shape_a)
    _, _, h_re, h_im = lax.associative_scan(
        _complex_affine_combine, (a_re_t, a_im_t, bu_re, bu_im), axis=1)
    y = (jnp.einsum('blgp,gcp->blgc', h_re, c_re)
         - jnp.einsum('blgp,gcp->blgc', h_im, c_im))
    y = y.reshape(bsz, seq, S5_WIDTH) + d_skip * u
    y = jax.nn.gelu(y)
    return y * jax.nn.sigmoid(y @ w_glu)


def _gla_mixer(q, k, v, r, gz, w_gate_up, b_gate, norm_g):
    bsz, seq, _ = q.shape
    f32 = jnp.float32
    n_chunks = seq // GLA_CHUNK
    q, k, v, r, gz = (t.astype(f32) for t in (q, k, v, r, gz))
    g = jax.nn.log_sigmoid(gz @ w_gate_up + b_gate) / GLA_TAU

    def chunks(t, d):
        return t.reshape(bsz, n_chunks, GLA_CHUNK, GLA_HEADS, d).transpose(0, 3, 1, 2, 4)

    qc = chunks(q, GLA_DK) * (GLA_DK ** -0.5)
    kc = chunks(k, GLA_DK)
    gc = chunks(g, GLA_DK)
    vc = chunks(v, GLA_DV)
    bcum = jnp.cumsum(gc, axis=3)
    blast = bcum[:, :, :, -1:, :]
    qe = qc * jnp.exp(bcum)
    ke = kc * jnp.exp(-bcum)
    mask = jnp.tril(jnp.ones((GLA_CHUNK, GLA_CHUNK), dtype=bool))
    scores = jnp.where(mask, jnp.einsum('bhnid,bhnjd->bhnij', qe, ke), 0.0)
    o_intra = jnp.einsum('bhnij,bhnje->bhnie', scores, vc)
    kd = kc * jnp.exp(blast - bcum)
    upd = jnp.einsum('bhncd,bhnce->nbhde', kd, vc)
    decay = jnp.exp(blast[:, :, :, 0, :]).transpose(2, 0, 1, 3)

    def step(state, inp):
        dec, u = inp
        return dec[..., None] * state + u, state

    s0 = jnp.zeros((bsz, GLA_HEADS, GLA_DK, GLA_DV), f32)
    _, s_prev = lax.scan(step, s0, (decay, upd))
    o_inter = jnp.einsum('bhncd,nbhde->bhnce', qe, s_prev)
    o = (o_intra + o_inter).transpose(0, 2, 3, 1, 4).reshape(bsz, seq, GLA_HEADS, GLA_DV)
    o = o * lax.rsqrt(jnp.mean(jnp.square(o), axis=-1, keepdims=True) + RMS_EPS) * norm_g
    return o.reshape(bsz, seq, GLA_WIDTH) * jax.nn.silu(r)


def _rglru_mixer(xb, gb, conv_w, conv_b, w_r, b_r, w_i, b_i, lam):
    bsz, seq, _ = xb.shape
    f32 = jnp.float32
    xb = xb.astype(f32)
    gb = gb.astype(f32)
    xc = lax.conv_general_dilated(
        xb, conv_w.astype(f32)[:, None, :], window_strides=(1,),
        padding=((LRU_CONV - 1, 0),), dimension_numbers=('NWC', 'WIO', 'NWC'),
        feature_group_count=LRU_WIDTH) + conv_b
    xblk = xc.reshape(bsz, seq, LRU_BLOCKS, LRU_BLOCK)
    gate_r = jax.nn.sigmoid(
        jnp.einsum('blhi,hij->blhj', xblk, w_r).reshape(bsz, seq, LRU_WIDTH) + b_r)
    gate_i = jax.nn.sigmoid(
        jnp.einsum('blhi,hij->blhj', xblk, w_i).reshape(bsz, seq, LRU_WIDTH) + b_i)
    log_a = -LRU_C * gate_r * jax.nn.softplus(-lam)
    a = jnp.exp(log_a)
    bterm = jnp.sqrt(-jnp.expm1(2.0 * log_a)) * (gate_i * xc)
    _, h = lax.associative_scan(_real_affine_combine, (a, bterm), axis=1)
    return h * jax.nn.gelu(gb)


def setup_inputs(seed: int = 0) -> dict:
    key = jax.random.key(seed)
    ks = jax.random.split(key, 32)
    f32 = jnp.float32
    L = DEPTH

    def nrm(k, shape, scale):
        return scale * jax.random.normal(k, shape, f32)

    x = jax.random.normal(ks[0], (BATCH, SEQ, D_MODEL), f32)
    ln_in_g = 1.0 + nrm(ks[1], (D_MODEL,), 0.02)
    ln_in_b = nrm(ks[2], (D_MODEL,), 0.02)
    w_in = nrm(ks[3], (L, D_MODEL, D_IN), D_MODEL ** -0.5)
    s5_lambda_re = -0.5 + nrm(ks[4], (L, S5_GROUPS, S5_STATE), 0.01)
    s5_lambda_im = (jnp.pi * jnp.arange(S5_STATE, dtype=f32)[None, None, :]
                    + nrm(ks[5], (L, S5_GROUPS, S5_STATE), 0.01))
    s5_log_dt = jax.random.uniform(ks[6], (L, S5_GROUPS), f32,
                                   minval=math.log(1e-3), maxval=math.log(1e-1))
    s5_b_re = nrm(ks[7], (L, S5_GROUPS, S5_STATE, S5_GROUP), (2 * S5_GROUP) ** -0.5)
    s5_b_im = nrm(ks[8], (L, S5_GROUPS, S5_STATE, S5_GROUP), (2 * S5_GROUP) ** -0.5)
    s5_c_re = nrm(ks[9], (L, S5_GROUPS, S5_GROUP, S5_STATE), (2 * S5_STATE) ** -0.5)
    s5_c_im = nrm(ks[10], (L, S5_GROUPS, S5_GROUP, S5_STATE), (2 * S5_STATE) ** -0.5)
    s5_d = nrm(ks[11], (L, S5_WIDTH), 1.0)
    s5_w_glu = nrm(ks[12], (L, S5_WIDTH, S5_WIDTH), S5_WIDTH ** -0.5)
    gla_w_gate_up = nrm(ks[13], (L, GLA_GATE_RANK, GLA_KEY_WIDTH), GLA_GATE_RANK ** -0.5)
    gla_b_gate = nrm(ks[14], (L, GLA_KEY_WIDTH), 0.02)
    gla_norm_g = 1.0 + nrm(ks[15], (L, GLA_DV), 0.02)
    lru_conv_w = nrm(ks[16], (L, LRU_CONV, LRU_WIDTH), LRU_CONV ** -0.5)
    lru_conv_b = nrm(ks[17], (L, LRU_WIDTH), 0.02)
    lru_w_r = nrm(ks[18], (L, LRU_BLOCKS, LRU_BLOCK, LRU_BLOCK), LRU_BLOCK ** -0.5)
    lru_b_r = nrm(ks[19], (L, LRU_WIDTH), 0.02)
    lru_w_i = nrm(ks[20], (L, LRU_BLOCKS, LRU_BLOCK, LRU_BLOCK), LRU_BLOCK ** -0.5)
    lru_b_i = nrm(ks[21], (L, LRU_WIDTH), 0.02)
    u = jax.random.uniform(ks[22], (L, LRU_WIDTH), f32, minval=0.9, maxval=0.999)
    p = u ** (1.0 / LRU_C)
    lru_lambda = jnp.log(p) - jnp.log1p(-p)
    w_out = nrm(ks[23], (L, D_MIX, D_MODEL), DEEPNORM_BETA * D_MIX ** -0.5)
    ln1_g = 1.0 + nrm(ks[24], (L, D_MODEL), 0.02)
    ln1_b = nrm(ks[25], (L, D_MODEL), 0.02)
    mlp_w1 = nrm(ks[26], (L, D_MODEL, D_FF), D_MODEL ** -0.5)
    mlp_w2 = nrm(ks[27], (L, D_FF, D_MODEL), DEEPNORM_BETA * D_FF ** -0.5)
    ln2_g = 1.0 + nrm(ks[28], (L, D_MODEL), 0.02)
    ln2_b = nrm(ks[29], (L, D_MODEL), 0.02)
    return {
        'x': x, 'ln_in_g': ln_in_g, 'ln_in_b': ln_in_b, 'w_in': w_in,
        's5_lambda_re': s5_lambda_re, 's5_lambda_im': s5_lambda_im, 's5_log_dt': s5_log_dt,
        's5_b_re': s5_b_re, 's5_b_im': s5_b_im, 's5_c_re': s5_c_re, 's5_c_im': s5_c_im,
        's5_d': s5_d, 's5_w_glu': s5_w_glu,
        'gla_w_gate_up': gla_w_gate_up, 'gla_b_gate': gla_b_gate, 'gla_norm_g': gla_norm_g,
        'lru_conv_w': lru_conv_w, 'lru_conv_b': lru_conv_b, 'lru_w_r': lru_w_r,
        'lru_b_r': lru_b_r, 'lru_w_i': lru_w_i, 'lru_b_i': lru_b_i, 'lru_lambda': lru_lambda,
        'w_out': w_out, 'ln1_g': ln1_g, 'ln1_b': ln1_b,
        'mlp_w1': mlp_w1, 'mlp_w2': mlp_w2, 'ln2_g': ln2_g, 'ln2_b': ln2_b,
    }


def reference(x, ln_in_g, ln_in_b, w_in, s5_lambda_re, s5_lambda_im, s5_log_dt,
              s5_b_re, s5_b_im, s5_c_re, s5_c_im, s5_d, s5_w_glu,
              gla_w_gate_up, gla_b_gate, gla_norm_g,
              lru_conv_w, lru_conv_b, lru_w_r, lru_b_r, lru_w_i, lru_b_i, lru_lambda,
              w_out, ln1_g, ln1_b, mlp_w1, mlp_w2, ln2_g, ln2_b):
    split_points = tuple(int(s) for s in np.cumsum(SPLIT_SIZES)[:-1])
    h = _layer_norm(x, ln_in_g, ln_in_b)
    for l in range(DEPTH):
        z = h @ w_in[l]
        s5_u, g_q, g_k, g_v, g_r, g_z, lru_x, lru_g = jnp.split(z, split_points, axis=-1)
        y_s5 = _s5_mixer(s5_u, s5_lambda_re[l], s5_lambda_im[l], s5_log_dt[l],
                         s5_b_re[l], s5_b_im[l], s5_c_re[l], s5_c_im[l], s5_d[l], s5_w_glu[l])
        y_gla = _gla_mixer(g_q, g_k, g_v, g_r, g_z, gla_w_gate_up[l], gla_b_gate[l],
                           gla_norm_g[l])
        y_lru = _rglru_mixer(lru_x, lru_g, lru_conv_w[l], lru_conv_b[l], lru_w_r[l],
                             lru_b_r[l], lru_w_i[l], lru_b_i[l], lru_lambda[l])
        mix = jnp.concatenate([y_s5, y_gla, y_lru], axis=-1) @ w_out[l]
        h = _layer_norm(DEEPNORM_ALPHA * h + mix, ln1_g[l], ln1_b[l])
        ff = jnp.square(jax.nn.relu(h @ mlp_w1[l])) @ mlp_w2[l]
        h = _layer_norm(DEEPNORM_ALPHA * h + ff, ln2_g[l], ln2_b[l])
    return h
```

```python
import contextlib
import numpy as np
import concourse.bass as bass
import concourse.mybir as mybir

F32 = mybir.dt.float32
BF16 = mybir.dt.bfloat16
I32 = mybir.dt.int32
ACT = mybir.ActivationFunctionType
ALU = mybir.AluOpType

ENGS = ('pe', 'dve', 'act', 'pool', 'sp')


class Buf:
    def __init__(self, t, parts=1, name=''):
        self.t = t
        self.name = name
        self.n = parts
        self.last_w = [None] * parts
        self.excl = False
        self.last_mm = None
        self.last_rg = None
        self.readers = [[] for _ in range(parts)]

    def __getitem__(self, k):
        return self.t[k]


class DmaGroup:
    def __init__(self, sem):
        self.sem = sem
        self.count = 0


class Instr:
    __slots__ = ('eng', 'fn', 'waits', 'signal', 'idx', 'clock', 'is_dma', 'group', 'final_waits')

    def __init__(self, eng, fn):
        self.eng = eng
        self.fn = fn
        self.waits = []
        self.signal = False
        self.is_dma = False
        self.group = None


def _norm(acc):
    out = []
    for a in acc:
        if isinstance(a, Buf):
            out.append((a, range(a.n)))
        else:
            b, p = a
            if isinstance(p, int):
                p = (p,)
            out.append((b, p))
    return out


class Sched:
    def __init__(self, nc, stack):
        self.nc = nc
        self.stack = stack
        self.streams = {e: [] for e in ENGS}
        self.know = {e: {f: -1 for f in ENGS} for e in ENGS}
        self.know_dma = {e: {} for e in ENGS}
        self.self_sync = {e: -1 for e in ENGS}
        self.sems = {e: stack.enter_context(nc.semaphore('sem_' + e)) for e in ENGS if e != 'sp'}
        self.groups = []
        self.nbuf = 0

    def sbuf(self, name, shape, dtype, parts=1):
        t = self.stack.enter_context(self.nc.sbuf_tensor(name, list(shape), dtype))
        return Buf(t, parts, name)

    def psum(self, name, shape, dtype, parts=1):
        t = self.stack.enter_context(self.nc.psum_tensor(name, list(shape), dtype))
        b = Buf(t, parts, name)
        b.excl = True
        return b

    def dram(self, name, shape, dtype, kind=None, parts=1):
        if kind is None:
            t = self.nc.dram_tensor(name, list(shape), dtype)
        else:
            t = self.nc.dram_tensor(name, list(shape), dtype, kind=kind)
        return Buf(t.ap(), parts, name)

    def group(self, name):
        g = DmaGroup(self.stack.enter_context(self.nc.semaphore('dg_' + name)))
        self.groups.append(g)
        return g

    def _need(self, ins, J, raw, force=False):
        E = ins.eng
        if J is None or J is ins:
            return
        if J.is_dma:
            g = J.group
            val = g.count
            if self.know_dma[E].get(g, 0) < val:
                ins.waits.append(('dma', g, val))
                self.know_dma[E][g] = val
            return
        F = J.eng
        if F == E:
            if E == 'pe' or not raw:
                return
            if self.self_sync[E] >= J.idx:
                return
            self.self_sync[E] = J.idx
            ins.waits.append(('eng', F, J.idx))
            J.signal = True
            return
        if self.know[E][F] >= J.idx:
            return
        ins.waits.append(('eng', F, J.idx))
        J.signal = True
        k = self.know[E]
        for f, v in J.clock.items():
            if k[f] < v:
                k[f] = v
        if k[F] < J.idx:
            k[F] = J.idx

    def add(self, eng, fn, reads=(), writes=(), group=None, rg=None):
        ins = Instr(eng, fn)
        reads = _norm(reads)
        writes = _norm(writes)
        if rg is not None:
            for b, ps in writes:
                if b.excl:
                    b.last_mm = ins
                    b.last_rg = rg
        if group is not None:
            ins.is_dma = True
            ins.group = group
        for b, ps in reads:
            for p in ps:
                self._need(ins, b.last_w[p], True)
                if b.excl:
                    for r in b.readers[p]:
                        self._need(ins, r, False)
        for b, ps in writes:
            for p in ps:
                self._need(ins, b.last_w[p], False)
                for r in b.readers[p]:
                    self._need(ins, r, False)
        st = self.streams[eng]
        ins.idx = len(st)
        st.append(ins)
        if group is not None:
            group.count += 16
        else:
            self.know[eng][eng] = ins.idx
        ins.clock = dict(self.know[eng])
        for b, ps in reads:
            for p in ps:
                b.readers[p].append(ins)
        for b, ps in writes:
            for p in ps:
                b.last_w[p] = ins
                b.readers[p] = []
        return ins

    def dma(self, eng, out, in_, reads, writes, group, **kw):
        return self.add(eng, lambda e: e.dma_start(out=out, in_=in_, **kw), reads, writes, group=group)

    def final_wait_all(self, eng='sp'):
        ins = Instr(eng, None)
        for g in self.groups:
            if g.count > 0:
                ins.waits.append(('dma', g, g.count))
        ins.idx = len(self.streams[eng])
        ins.clock = {}
        self.streams[eng].append(ins)

    def emit(self):
        nc = self.nc
        sigcount = {}
        for e in ENGS:
            c = 0
            arr = []
            for ins in self.streams[e]:
                if ins.signal:
                    c += 1
                arr.append(c)
            sigcount[e] = arr
        sems = self.sems

        def run(e, engh):
            for ins in self.streams[e]:
                for w in ins.waits:
                    if w[0] == 'dma':
                        engh.wait_ge(w[1].sem, w[2])
                    else:
                        engh.wait_ge(sems[w[1]], sigcount[w[1]][w[2]])
                if ins.fn is None:
                    continue
                r = ins.fn(engh)
                if ins.is_dma:
                    r.then_inc(ins.group.sem, 16)
                elif ins.signal:
                    r.then_inc(sems[e], 1)

        with nc.Block() as block:
            @block.tensor
            def _(t):
                run('pe', t)

            @block.vector
            def _(v):
                run('dve', v)

            @block.scalar
            def _(a):
                run('act', a)

            @block.gpsimd
            def _(g):
                run('pool', g)

            @block.sync
            def _(s):
                run('sp', s)

    def stats(self):
        return {e: len(self.streams[e]) for e in ENGS}
D = 1024
DIN = 2320
DFF = 4096
SEQ = 8192
T = 512
NKC = 8
ALPHA = 4 ** 0.25
LN_EPS = 1e-5
RMS_EPS = 1e-6
WIN_SLABS = [[(0, 512)], [(512, 768), (1792, 1808)], [(768, 1280)], [(1280, 1792)], [(1808, 2320)]]
NSLAB_L = 5 + 2 + 8 + 8
NSLOT = 2

PARAM_SHAPES = {
    'ln_in_g': (1024,), 'ln_in_b': (1024,), 'w_in': (2, 1024, 2320),
    's5_lambda_re': (2, 16, 64), 's5_lambda_im': (2, 16, 64), 's5_log_dt': (2, 16),
    's5_b_re': (2, 16, 64, 16), 's5_b_im': (2, 16, 64, 16), 's5_c_re': (2, 16, 16, 64),
    's5_c_im': (2, 16, 16, 64), 's5_d': (2, 256), 's5_w_glu': (2, 256, 256),
    'gla_w_gate_up': (2, 16, 256), 'gla_b_gate': (2, 256), 'gla_norm_g': (2, 128),
    'lru_conv_w': (2, 4, 256), 'lru_conv_b': (2, 256), 'lru_w_r': (2, 8, 32, 32),
    'lru_b_r': (2, 256), 'lru_w_i': (2, 8, 32, 32), 'lru_b_i': (2, 256), 'lru_lambda': (2, 256),
    'w_out': (2, 1024, 1024), 'ln1_g': (2, 1024), 'ln1_b': (2, 1024),
    'mlp_w1': (2, 1024, 4096), 'mlp_w2': (2, 4096, 1024), 'ln2_g': (2, 1024), 'ln2_b': (2, 1024),
}


class K:
    pass


def build(n_tiles=16, layers=(0, 1), ln_in=True, out_tok=True, dbg=None, seq=SEQ):
    nc = bass.Bass('TRN2', target_bir_lowering=False)
    with contextlib.ExitStack() as stack:
        s = Sched(nc, stack)
        k = K()
        k.s, k.nc, k.dbg, k.dbg_outs = s, nc, dbg, []
        k.layers = layers
        k.x = s.dram('x', (seq, D), F32, kind='ExternalInput')
        k.P = {n: s.dram(n, shp, F32, kind='ExternalInput') for n, shp in PARAM_SHAPES.items()}
        k.out = s.dram('out', (seq, D), F32, kind='ExternalOutput')
        k.wsc = s.dram('wsc', (2 * NSLAB_L, 128, 4096), BF16, parts=2 * NSLAB_L)
        alloc(k)
        alloc2(k)
        consts(k)
        consts2(k)
        slab_iter_init(k, n_tiles)
        prologue_weights(k)
        for l in layers:
            layer_setup(k, l)
        for i in range(n_tiles):
            tile_front(k, i, ln_in)
            k.stop = dbg[0] if (dbg and dbg[0].startswith('stop_')) else None
            if k.stop == 'stop_front':
                continue
            for l in layers:
                tile_layer(k, i, l)
            if k.stop is None:
                tile_back(k, i, out_tok)
        s.final_wait_all('sp')
        s.emit()
        k.stats = s.stats()
    return nc, k


def dump(k, name, ap, shape, reads, dtype=F32):
    s = k.s
    d = s.dram('dbg_' + name, shape, dtype, kind='ExternalOutput')
    g = s.group('dbg_' + name)
    s.dma('sp', d.t, ap, reads, [d], g)
    k.dbg_outs.append('dbg_' + name)


def alloc(k):
    s = k.s
    k.ps = [s.psum('ps%d' % i, (128, 512), F32) for i in range(8)]
    k.psi = 0
    k.slots = [s.sbuf('wslot%d' % i, (128, 4096), BF16) for i in range(NSLOT)]
    k.slot_g = [s.group('slot%d' % i) for i in range(NSLOT)]
    k.slab_seq = 0
    k.cgrp = s.group('const')
    k.xin_g = s.group('xin')
    k.h = s.sbuf('h', (128, NKC, T), F32, parts=NKC)
    k.hb = s.sbuf('hb', (128, NKC, T), BF16, parts=NKC)
    k.ident = s.sbuf('ident', (128, 128), F32)
    k.lnp = s.sbuf('lnp', (128, 2 + 8, NKC), F32)
    k.stat = s.sbuf('stat', (128, 4, 16), F32, parts=4)
    k.evi = 0


def nextps(k):
    while True:
        p = k.ps[k.psi % 8]
        k.psi += 1
        if p not in getattr(k, 'ps_reserved', ()):
            return p


def consts(k):
    s = k.s
    idn = k.ident
    s.add('pool', lambda e: e.memset(idn[:], 1.0), [], [idn])
    s.add('pool', lambda e: e.affine_select(out=idn[:], in_=idn[:], pattern=[[-1, 128]], compare_op=ALU.is_equal,
                                            fill=0.0, base=0, channel_multiplier=1), [idn], [idn])
    def ld(dst, src):
        s.dma('sp', dst, src.rearrange('(kc p) -> p kc', p=128), [], [k.lnp], k.cgrp,
              allow_slow_non_contiguous=True)
    ld(k.lnp[:, 0, :], k.P['ln_in_g'].t)
    ld(k.lnp[:, 1, :], k.P['ln_in_b'].t)
    for l in range(2):
        for j, n in enumerate(['ln1_g', 'ln1_b', 'ln2_g', 'ln2_b']):
            ld(k.lnp[:, 2 + 4 * l + j, :], k.P[n].t[l])


def slab_src_ranges(k, l, j):
    P = k.P
    if j < 5:
        src = P['w_in'].t[l].rearrange('(kc p) n -> p kc n', p=128)
        out, off = [], 0
        for (c0, c1) in WIN_SLABS[j]:
            out.append((off, c1 - c0, src[:, :, c0:c1]))
            off += c1 - c0
        return 8, out
    if j < 7:
        src = P['w_out'].t[l].rearrange('(kc p) n -> p kc n', p=128)
        c0 = (j - 5) * 512
        return 8, [(0, 512, src[:, :, c0:c0 + 512])]
    if j < 15:
        src = P['mlp_w1'].t[l].rearrange('(kc p) n -> p kc n', p=128)
        c0 = (j - 7) * 512
        return 8, [(0, 512, src[:, :, c0:c0 + 512])]
    src = P['mlp_w2'].t[l].rearrange('(kc p) n -> p kc n', p=128)
    c0 = (j - 15) * 128
    return 32, [(0, 128, src[:, :, c0:c0 + 128])]


def prologue_weights(k):
    s = k.s
    stg_g = [s.group('stg0'), s.group('stg1')]
    hid32, _ = AV(k, 0, 16)
    cast_engs = ['dve', 'act', 'act']
    n = 0
    for l in k.layers:
        for j in range(NSLAB_L):
            half = n % 2
            parts = list(range(8 * half, 8 * half + 8))
            nkc, rngs = slab_src_ranges(k, l, j)
            w = 4096 // nkc
            stg = hid32[:, half * 4096:(half + 1) * 4096].rearrange('p (kc n) -> p kc n', kc=nkc)
            for (off, ncols, src) in rngs:
                s.dma('sp', stg[:, :, off:off + ncols], src, [], [(k.A, parts)], stg_g[half])
            slot = k.slots[n % NSLOT]
            ce = cast_engs[n % 3]
            stg_flat = hid32[:, half * 4096:(half + 1) * 4096]
            if ce == 'act':
                s.add('act', lambda e, o=slot, i=stg_flat: e.activation(out=o[:], in_=i, func=ACT.Copy),
                      [(k.A, parts)], [slot])
            else:
                s.add(ce, lambda e, o=slot, i=stg_flat: e.tensor_copy(out=o[:], in_=i),
                      [(k.A, parts)], [slot])
            sid = l * NSLAB_L + j
            s.dma('act', k.wsc.t[sid], slot[:], [slot], [(k.wsc, sid)], k.slot_g[n % NSLOT])
            n += 1


def load_slab(k, l, j):
    s = k.s
    q = k.slab_seq % NSLOT
    k.slab_seq += 1
    slot = k.slots[q]
    sid = l * NSLAB_L + j
    s.dma('sp', slot[:], k.wsc.t[sid], [(k.wsc, sid)], [slot], k.slot_g[q])
    return slot


def evac_engine(k):
    k.evi += 1
    return 'act' if k.evi % 2 else 'dve'


def tile_front(k, i, ln_in):
    s = k.s
    h, hb = k.h, k.hb
    xin_, _ = AV(k, 0, 8)
    xin = xin_.rearrange('p (tb d) -> p tb d', tb=4)
    xp = lambda tb: (k.A, [2 * tb, 2 * tb + 1])
    xall = (k.A, list(range(0, 8)))
    src = k.x.t[i * T:(i + 1) * T, :].rearrange('(tb p) d -> p tb d', p=128)
    s.dma('sp', xin, src, [], [xall], k.xin_g)
    st = k.stat
    if ln_in:
        for tb in range(4):
            s.add('dve', lambda e, tb=tb: e.bn_stats(out=st[:, tb, 0:6], in_=xin[:, tb, 0:512]), [xp(tb)], [(st, tb)])
            s.add('dve', lambda e, tb=tb: e.bn_stats(out=st[:, tb, 6:12], in_=xin[:, tb, 512:1024]), [xp(tb)], [(st, tb)])
        for tb in range(4):
            s.add('dve', lambda e, tb=tb: e.bn_aggr(out=st[:, tb, 12:14], in_=st[:, tb, 0:12]), [(st, tb)], [(st, tb)])
        s.add('act', lambda e: e.activation(out=st[:, :, 14], in_=st[:, :, 13], func=ACT.Ln, bias=LN_EPS, scale=1.0), [st], [st])
        s.add('act', lambda e: e.activation(out=st[:, :, 15], in_=st[:, :, 14], func=ACT.Exp, scale=-0.5), [st], [st])
        for tb in range(4):
            s.add('dve', lambda e, tb=tb: e.tensor_scalar(out=xin[:, tb, :], in0=xin[:, tb, :], scalar1=st[:, tb, 12:13],
                                                          scalar2=st[:, tb, 15:16], op0=ALU.subtract, op1=ALU.mult),
                  [xp(tb), (st, tb)], [xp(tb)])
    for kc in range(NKC):
        ps = nextps(k)
        for tb in range(4):
            s.add('pe', lambda e, tb=tb, kc=kc, ps=ps: e.transpose(out=ps[:, tb * 128:(tb + 1) * 128],
                                                                   in_=xin[:, tb, kc * 128:(kc + 1) * 128],
                                                                   identity=k.ident[:]),
                  [xp(tb), k.ident], [ps])
        if ln_in:
            s.add('act', lambda e, kc=kc, ps=ps: e.activation(out=h[:, kc, :], in_=ps[:], func=ACT.Identity,
                                                              scale=k.lnp[:, 0, kc:kc + 1], bias=k.lnp[:, 1, kc:kc + 1]),
                  [ps, k.lnp], [(h, kc)])
        else:
            s.add('act', lambda e, kc=kc, ps=ps: e.activation(out=h[:, kc, :], in_=ps[:], func=ACT.Copy),
                  [ps], [(h, kc)])
        s.add('dve', lambda e, kc=kc: e.tensor_copy(out=hb[:, kc, :], in_=h[:, kc, :]), [(h, kc)], [(hb, kc)])
    if k.dbg and ('h0' in k.dbg[0].split(',')) and i == 0:
        dump(k, 'h0', h[:], (128, NKC, T), [h])


def tile_back(k, i, out_tok):
    s = k.s
    h = k.h
    xin_, _ = BV(k, 0, 8)
    xin = xin_.rearrange('p (tb d) -> p tb d', tb=4)
    xp = lambda tb: (k.B, [2 * tb, 2 * tb + 1])
    if not hasattr(k, 'out_g'):
        k.out_g = s.group('outst')
    for tb in range(4):
        for half in range(2):
            ps = nextps(k)
            for q in range(4):
                kc = half * 4 + q
                s.add('pe', lambda e, tb=tb, kc=kc, q=q, ps=ps: e.transpose(out=ps[:, q * 128:(q + 1) * 128],
                                                                            in_=h[:, kc, tb * 128:(tb + 1) * 128],
                                                                            identity=k.ident[:]),
                      [(h, kc), k.ident], [ps])
            eng = evac_engine(k)
            if eng == 'act':
                s.add('act', lambda e, tb=tb, half=half, ps=ps: e.activation(out=xin[:, tb, half * 512:(half + 1) * 512],
                                                                            in_=ps[:], func=ACT.Copy), [ps], [xp(tb)])
            else:
                s.add('dve', lambda e, tb=tb, half=half, ps=ps: e.tensor_copy(out=xin[:, tb, half * 512:(half + 1) * 512],
                                                                              in_=ps[:]), [ps], [xp(tb)])
    dst = k.out.t[i * T:(i + 1) * T, :].rearrange('(tb p) d -> p tb d', p=128)
    s.dma('act', dst, xin, [(k.B, list(range(8)))], [k.out], k.out_g)
def TT(k, eng, out, in0, in1, op, R, W):
    k.s.add(eng, lambda e: e.tensor_tensor(out=out, in0=in0, in1=in1, op=op), R, W)


def TS(k, eng, out, in0, s1, s2, op0, op1, R, W):
    if s2 is None:
        k.s.add(eng, lambda e: e.tensor_scalar(out=out, in0=in0, scalar1=s1, scalar2=None, op0=op0), R, W)
    else:
        k.s.add(eng, lambda e: e.tensor_scalar(out=out, in0=in0, scalar1=s1, scalar2=s2, op0=op0, op1=op1), R, W)


def STT(k, out, in0, sc, in1, op0, op1, R, W):
    k.s.add('dve', lambda e: e.scalar_tensor_tensor(out=out, in0=in0, scalar=sc, in1=in1, op0=op0, op1=op1), R, W)


def AF(k, out, in_, func, R, W, bias=None, scale=None):
    kw = {}
    if bias is not None:
        kw['bias'] = bias
    if scale is not None:
        kw['scale'] = scale
    k.s.add('act', lambda e: e.activation(out=out, in_=in_, func=func, **kw), R, W)


def CP(k, eng, out, in_, R, W):
    if eng == 'act':
        k.s.add('act', lambda e: e.activation(out=out, in_=in_, func=ACT.Copy), R, W)
    else:
        k.s.add(eng, lambda e: e.tensor_copy(out=out, in_=in_), R, W)


def MM(k, out, lhsT, rhs, start, stop, R, W):
    rows = lhsT.shape[0]
    rr = 32 if rows <= 32 else (64 if rows <= 64 else 128)
    rg = 'full' if rr == 128 else (lhsT.base_partition(), rr)
    k.s.add('pe', lambda e: e.matmul(out, lhsT, rhs, start=start, stop=stop), R, W, rg=rg)


def MS(k, eng, ap, val, W):
    k.s.add(eng, lambda e: e.memset(ap, val), [], W)


def ldc(k, dst, src, W):
    k.s.dma('sp', dst, src, [], W, k.cgrp, allow_slow_non_contiguous=True)


def AV(k, c0, n, dtype=F32):
    ap = k.A.t[:, c0:c0 + n, :].rearrange('p a b -> p (a b)')
    if dtype == BF16:
        ap = ap.bitcast(BF16)
    return ap, (k.A, list(range(c0, c0 + n)))


def BV(k, c0, n, dtype=F32):
    ap = k.B.t[:, c0:c0 + n, :].rearrange('p a b -> p (a b)')
    if dtype == BF16:
        ap = ap.bitcast(BF16)
    return ap, (k.B, list(range(c0, c0 + n)))


def alloc2(k):
    s = k.s
    k.A = s.sbuf('arenaA', (128, 24, 512), F32, parts=24)
    k.B = s.sbuf('arenaB', (128, 12, 512), F32, parts=12)
    k.identb = s.sbuf('identb', (128, 128), BF16)
    k.onesD = s.sbuf('onesD', (128, 128), BF16)
    k.onesV = s.sbuf('onesV', (128, 128), BF16)
    k.cmask = s.sbuf('cmask', (128, 512), F32)
    k.tri = s.sbuf('tri', (128, 64), F32)
    k.lx = s.sbuf('lx', (128, 2, 515), F32, parts=2)
    k.shiftI = s.sbuf('shiftI', (128, 384), BF16)
    k.L = {}


def consts2(k):
    s = k.s
    CP(k, 'dve', k.identb[:], k.ident[:], [k.ident], [k.identb])
    MS(k, 'pool', k.onesD[:], 1.0 / 1024, [k.onesD])
    MS(k, 'pool', k.onesV[:], 1.0 / 128, [k.onesV])
    MS(k, 'pool', k.cmask[:], 1.0, [k.cmask])
    MS(k, 'pool', k.cmask[:].rearrange('p (n c) -> p n c', c=64)[:, :, 0:1], 0.0, [k.cmask])
    MS(k, 'pool', k.shiftI[:], 0.0, [k.shiftI])
    CP(k, 'pool', k.shiftI[:, 128:256], k.ident[:], [k.ident], [k.shiftI])
    MS(k, 'pool', k.tri[:], 1.0, [k.tri])
    for hf in range(2):
        s.add('pool', lambda e, hf=hf: e.affine_select(out=k.tri[hf * 64:(hf + 1) * 64, :], in_=k.tri[hf * 64:(hf + 1) * 64, :],
                                                       pattern=[[1, 64]], compare_op=ALU.is_ge, fill=0.0, base=0,
                                                       channel_multiplier=-1), [k.tri], [k.tri])


def cmul(k, o_r, o_i, a_r, a_i, b_r, b_i, t1, t2, R, W):
    TT(k, 'dve', t1, a_r, b_r, ALU.mult, R, W)
    TT(k, 'dve', t2, a_i, b_i, ALU.mult, R, W)
    TT(k, 'dve', o_r, t1, t2, ALU.subtract, R, W)
    TT(k, 'dve', t1, a_r, b_i, ALU.mult, R, W)
    TT(k, 'dve', t2, a_i, b_r, ALU.mult, R, W)
    TT(k, 'dve', o_i, t1, t2, ALU.add, R, W)


TROWS = [96, 96, 64]
TPAIRS = [3, 3, 2]


def Lc_tmpb(k):
    ap = k.B.t[:, 11, :].bitcast(BF16)
    Lmb = ap[:, 0:512].rearrange('p (r a c) -> p r a c', r=2, a=8)
    CAb = ap[:, 512:1024].rearrange('p (r a c) -> p r a c', r=2, a=8)
    return Lmb, CAb


def layer_setup(k, l):
    s = k.s
    P = k.P
    Lc = K()
    k.L[l] = Lc
    n = 'L%d_' % l
    Lc.Wx = s.sbuf(n + 'Wx', (128, 3, 8, 2, 128), BF16)
    Lc.WC = s.sbuf(n + 'WC', (128, 8, 8, 2, 32), BF16)
    Lc.Wt = s.sbuf(n + 'Wt', (128, 3, 8, 128), BF16)
    Lc.E = s.sbuf(n + 'E', (128, 2, 8, 64), F32)
    Lc.rho = s.sbuf(n + 'rho', (128, 8, 64), F32)
    Lc.rho1 = s.sbuf(n + 'rho1', (128, 8), F32)
    Lc.sm = s.sbuf(n + 'sm', (128, 32), F32)
    Lc.wglu = s.sbuf(n + 'wglu', (128, 2, 256), BF16)
    Lc.sd = s.sbuf(n + 'sd', (128, 3), F32)
    Lc.wup = s.sbuf(n + 'wup', (16, 256), BF16)
    Lc.lgw = s.sbuf(n + 'lgw', (128, 2, 2, 128), BF16)
    Lc.Sf = s.sbuf(n + 'Sf', (128, 2, 8, 65), F32)
    Lc.Sg = s.sbuf(n + 'Sg', (128, 2, 128), F32)
    Lc.lh = s.sbuf(n + 'lh', (128, 2), F32)
    Lc.dec = s.sbuf(n + 'dec', (128, 16), F32)
    Lc.lxh = s.sbuf(n + 'lxh', (128, 2, 3), F32)
    sm = Lc.sm
    A = k.B
    allA = [A]
    for ti in range(3):
        ldc(k, Lc.sd[0:TROWS[ti], ti:ti + 1], P['s5_d'].t[l][96 * ti:96 * ti + TROWS[ti]].rearrange('(p o) -> p o', o=1), [Lc.sd])
    ldc(k, sm[:, 2:4], P['gla_b_gate'].t[l].rearrange('(ct p) -> p ct', p=128), [sm])
    ldc(k, sm[:, 4:5], P['gla_norm_g'].t[l].rearrange('(o p) -> p o', p=128), [sm])
    for ct in range(2):
        ldc(k, sm[:, 5 + 4 * ct:9 + 4 * ct], P['lru_conv_w'].t[l][:, ct * 128:(ct + 1) * 128].rearrange('k p -> p k'), [sm])
    ldc(k, sm[:, 13:15], P['lru_conv_b'].t[l].rearrange('(ct p) -> p ct', p=128), [sm])
    ldc(k, sm[:, 15:17], P['lru_b_r'].t[l].rearrange('(ct p) -> p ct', p=128), [sm])
    ldc(k, sm[:, 17:19], P['lru_b_i'].t[l].rearrange('(ct p) -> p ct', p=128), [sm])
    ldc(k, sm[:, 19:21], P['lru_lambda'].t[l].rearrange('(ct p) -> p ct', p=128), [sm])
    TS(k, 'dve', sm[:, 2:4], sm[:, 2:4], -1.0, None, ALU.mult, None, [sm], [sm])
    AF(k, sm[:, 21:23], sm[:, 19:21], ACT.Exp, [sm], [sm], scale=-1.0)
    AF(k, sm[:, 21:23], sm[:, 21:23], ACT.Ln, [sm], [sm], bias=1.0)
    TS(k, 'dve', sm[:, 19:21], sm[:, 21:23], -8.0, None, ALU.mult, None, [sm], [sm])
    MS(k, 'pool', Lc.Sf[:], 0.0, [Lc.Sf])
    MS(k, 'pool', Lc.Sg[:], 0.0, [Lc.Sg])
    MS(k, 'pool', Lc.lh[:], 0.0, [Lc.lh])
    MS(k, 'pool', Lc.lxh[:], 0.0, [Lc.lxh])
    a0, _ = BV(k, 0, 12)
    def seg(off, *shape):
        nel = int(np.prod(shape))
        ap = a0[:, off:off + nel]
        if len(shape) == 2:
            ap = ap.rearrange('p (a b) -> p a b', a=shape[0])
        elif len(shape) == 3:
            ap = ap.rearrange('p (a b c) -> p a b c', a=shape[0], b=shape[1])
        elif len(shape) == 4:
            ap = ap.rearrange('p (a b c d) -> p a b c d', a=shape[0], b=shape[1], c=shape[2])
        return ap, off + nel
    off = 0
    wst, off = seg(off, 2, 256)
    gst, off = seg(off, 256)
    lst, off = seg(off, 2, 2, 128)
    t8, off = seg(off, 24, 8)
    ti8, off = seg(off, 2, 8)
    Ap, off = seg(off, 2, 9, 8)
    Bn, off = seg(off, 2, 8, 16)
    Bb, off = seg(off, 2, 8, 16)
    BA, off = seg(off, 2, 8, 16)
    T1, off = seg(off, 8, 32)
    T2, off = seg(off, 8, 32)
    St, off = seg(off, 2, 8, 2, 16)
    Lm, off = seg(off, 2, 8, 16)
    Cn, off = seg(off, 2, 3, 64)
    Cr, off = seg(off, 2, 64)
    CT, off = seg(off, 2, 8, 2, 16)
    CA, off = seg(off, 2, 8, 32)
    U, off = seg(off, 2, 8)
    Pw, off = seg(off, 2, 8)
    Pw2, off = seg(off, 2, 8)
    assert off <= 11 * 512, off
    R = allA
    W = allA
    ldc(k, wst, P['s5_w_glu'].t[l].rearrange('(kc p) n -> p kc n', p=128), W)
    CP(k, 'dve', Lc.wglu[:], wst, R, [Lc.wglu])
    ldc(k, gst[0:16, :], P['gla_w_gate_up'].t[l], W)
    CP(k, 'dve', Lc.wup[:], gst[0:16, :], R, [Lc.wup])
    MS(k, 'pool', lst, 0.0, W)
    for gi_, nm in enumerate(['lru_w_r', 'lru_w_i']):
        for hb_ in range(8):
            ct, q = hb_ // 4, hb_ % 4
            ldc(k, lst[32 * q:32 * q + 32, ct, gi_, 32 * q:32 * q + 32], P[nm].t[l][hb_], W)
    CP(k, 'dve', Lc.lgw[:], lst, R, [Lc.lgw])
    lamr, lami, ldt = t8[:, 0, :], t8[:, 1, :], t8[:, 2, :]
    ldc(k, lamr, P['s5_lambda_re'].t[l].rearrange('(pr gi) p -> (gi p) pr', gi=2), W)
    ldc(k, lami, P['s5_lambda_im'].t[l].rearrange('(pr gi) p -> (gi p) pr', gi=2), W)
    ldtT = P['s5_log_dt'].t.tensor
    for gi in range(2):
        ldc(k, t8[gi * 64:(gi + 1) * 64, 2, :], bass.AP(ldtT, l * 16 + gi, [[0, 64], [2, 8]]), W)
    dt, lr, x1, mag, th = t8[:, 3, :], t8[:, 4, :], t8[:, 5, :], t8[:, 6, :], t8[:, 7, :]
    AF(k, dt, ldt, ACT.Exp, R, W)
    TS(k, 'dve', lr, lamr, -1e-4, None, ALU.min, None, R, W)
    TT(k, 'dve', x1, lr, dt, ALU.mult, R, W)
    AF(k, mag, x1, ACT.Exp, R, W)
    AF(k, Lc.rho1[:], x1, ACT.Exp, R, [A, Lc.rho1], scale=8.0)
    TT(k, 'dve', th, lami, dt, ALU.mult, R, W)
    cs = [t8[:, 8, :], t8[:, 9, :]]
    rr, fl, mk = t8[:, 10, :], t8[:, 11, :], t8[:, 12, :]
    for which in range(2):
        TS(k, 'dve', rr, th, 1.0 / (2 * np.pi), 0.25 if which == 0 else 0.0, ALU.mult, ALU.add, R, W)
        ii = ti8[:, 0, :].bitcast(I32)
        CP(k, 'dve', ii, rr, R, W)
        CP(k, 'dve', fl, ii, R, W)
        TT(k, 'dve', rr, rr, fl, ALU.subtract, R, W)
        TS(k, 'dve', mk, rr, 0.5, None, ALU.is_gt, None, R, W)
        TT(k, 'dve', rr, rr, mk, ALU.subtract, R, W)
        TS(k, 'dve', mk, rr, -0.5, None, ALU.is_lt, None, R, W)
        TT(k, 'dve', rr, rr, mk, ALU.add, R, W)
        AF(k, cs[which], rr, ACT.Sin, R, W, scale=2 * np.pi * (1 - 1e-6))
    MS(k, 'pool', Ap[:, 0, 0, :], 1.0, W)
    MS(k, 'pool', Ap[:, 1, 0, :], 0.0, W)
    TT(k, 'dve', Ap[:, 0, 1, :], mag, cs[0], ALU.mult, R, W)
    TT(k, 'dve', Ap[:, 1, 1, :], mag, cs[1], ALU.mult, R, W)
    tA, tB = t8[:, 13, :], t8[:, 14, :]
    for kk in range(2, 9):
        cmul(k, Ap[:, 0, kk, :], Ap[:, 1, kk, :], Ap[:, 0, kk - 1, :], Ap[:, 1, kk - 1, :], Ap[:, 0, 1, :], Ap[:, 1, 1, :], tA, tB, R, W)
    CP(k, 'dve', Pw[:, 0, :], cs[0], R, W)
    CP(k, 'dve', Pw[:, 1, :], cs[1], R, W)
    cur, nxt = Pw, Pw2
    for _ in range(3):
        cmul(k, nxt[:, 0, :], nxt[:, 1, :], cur[:, 0, :], cur[:, 1, :], cur[:, 0, :], cur[:, 1, :], tA, tB, R, W)
        cur, nxt = nxt, cur
    CP(k, 'dve', U[:, 0, :], cur[:, 0, :], R, W)
    CP(k, 'dve', U[:, 1, :], cur[:, 1, :], R, W)
    E = Lc.E
    RE, WE = [A, E], [A, E]
    CP(k, 'dve', E[:, 0, :, 0], U[:, 0, :], RE, WE)
    CP(k, 'dve', E[:, 1, :, 0], U[:, 1, :], RE, WE)
    m = 1
    while m < 64:
        br = cur[:, 0, :].unsqueeze(2).broadcast_to([128, 8, m])
        bi = cur[:, 1, :].unsqueeze(2).broadcast_to([128, 8, m])
        cmul(k, E[:, 0, :, m:2 * m], E[:, 1, :, m:2 * m], E[:, 0, :, 0:m], E[:, 1, :, 0:m], br, bi,
             T1[:, :, 0:m], T2[:, :, 0:m], RE, WE)
        cmul(k, nxt[:, 0, :], nxt[:, 1, :], cur[:, 0, :], cur[:, 1, :], cur[:, 0, :], cur[:, 1, :], tA, tB, R, W)
        cur, nxt = nxt, cur
        m *= 2
    CP(k, 'dve', Lc.rho[:], Lc.rho1[:].unsqueeze(2).broadcast_to([128, 8, 64]), [Lc.rho1], [Lc.rho])
    MS(k, 'pool', Lc.rho[:, :, 0:1], 0.0, [Lc.rho])
    am1, den, fr, fi = t8[:, 15, :], t8[:, 16, :], t8[:, 17, :], t8[:, 18, :]
    ar, ai = Ap[:, 0, 1, :], Ap[:, 1, 1, :]
    TS(k, 'dve', am1, ar, -1.0, None, ALU.add, None, R, W)
    TT(k, 'dve', den, lr, lr, ALU.mult, R, W)
    TT(k, 'dve', tA, lami, lami, ALU.mult, R, W)
    TT(k, 'dve', den, den, tA, ALU.add, R, W)
    k.s.add('dve', lambda e: e.reciprocal(out=den, in_=den), R, W)
    TT(k, 'dve', tA, am1, lr, ALU.mult, R, W)
    TT(k, 'dve', tB, ai, lami, ALU.mult, R, W)
    TT(k, 'dve', tA, tA, tB, ALU.add, R, W)
    TT(k, 'dve', fr, tA, den, ALU.mult, R, W)
    TT(k, 'dve', tA, ai, lr, ALU.mult, R, W)
    TT(k, 'dve', tB, am1, lami, ALU.mult, R, W)
    TT(k, 'dve', tA, tA, tB, ALU.subtract, R, W)
    TT(k, 'dve', fi, tA, den, ALU.mult, R, W)
    for ri, nm in enumerate(['s5_b_re', 's5_b_im']):
        ldc(k, Bn[:, ri, :, :], P[nm].t[l].rearrange('(pr gi) p c -> (gi p) pr c', gi=2), W)
    b16 = lambda ap: ap.unsqueeze(2).broadcast_to([128, 8, 16])
    T1s, T2s = T1[:, :, 0:16], T2[:, :, 0:16]
    cmul(k, Bb[:, 0], Bb[:, 1], Bn[:, 0], Bn[:, 1], b16(fr), b16(fi), T1s, T2s, R, W)
    MS(k, 'pool', St, 0.0, W)
    Lmb, CAb = Lc_tmpb(k)
    for j in range(8):
        kk = 7 - j
        cmul(k, BA[:, 0], BA[:, 1], Bb[:, 0], Bb[:, 1], b16(Ap[:, 0, kk, :]), b16(Ap[:, 1, kk, :]), T1s, T2s, R, W)
        for ri in range(2):
            for gi in range(2):
                CP(k, 'dve', St[gi * 64:(gi + 1) * 64, ri, :, gi, :], BA[gi * 64:(gi + 1) * 64, ri, :, :], R, W)
        if j == 7:
            CP(k, 'dve', Lmb, St.rearrange('p r a g c -> p r a (g c)'), R, W)
        for ri in range(2):
            for ti in range(3):
                ps = nextps(k)
                rw = TROWS[ti]
                in_ = St[:, ri, 3 * ti:3 * ti + TPAIRS[ti], :, :].rearrange('p a b c -> p (a b c)')
                k.s.add('pe', lambda e, ps=ps, in_=in_, rw=rw: e.transpose(out=ps[0:rw, 0:128], in_=in_, identity=k.ident[:]),
                        [A, k.ident], [ps])
                CP(k, 'act', Lc.Wx[0:rw, ti, j, ri, :], ps[0:rw, 0:128], [ps], [Lc.Wx])
    for ri, nm in enumerate(['s5_c_re', 's5_c_im']):
        for ti in range(3):
            ng = 2 * TPAIRS[ti]
            ldc(k, Cn[0:TROWS[ti], ri, ti, :], P[nm].t[l][6 * ti:6 * ti + ng].rearrange('s c p -> (s c) p'), W)
    MS(k, 'pool', CT, 0.0, W)
    for ri in range(2):
        for ti in range(3):
            rw, npr = TROWS[ti], TPAIRS[ti]
            for gi in range(2):
                CP(k, 'dve', Cr[0:rw, gi, :], Cn[0:rw, ri, ti, :], R, W)
            ps = nextps(k)
            in_ = Cr[0:rw].rearrange('p a b -> p (a b)')
            k.s.add('pe', lambda e, ps=ps, in_=in_, rw=rw: e.transpose(out=ps[:, 0:rw], in_=in_, identity=k.ident[0:rw, 0:rw]),
                    [A, k.ident], [ps])
            psv = ps[:, 0:rw].rearrange('p (q g c) -> p q g c', q=npr, g=2)
            for gi in range(2):
                CP(k, 'dve', CT[gi * 64:(gi + 1) * 64, ri, 3 * ti:3 * ti + npr, gi, :], psv[gi * 64:(gi + 1) * 64, :, gi, :], [ps], W)
    CTf = CT.rearrange('p r a g c -> p r a (g c)')
    b32 = lambda ap: ap.unsqueeze(2).broadcast_to([128, 8, 32])
    T1c, T2c = T1[:, :, 0:32], T2[:, :, 0:32]
    MS(k, 'pool', Lc.Wt[:], 0.0, [Lc.Wt])
    for kk in range(9):
        akr, aki = b32(Ap[:, 0, kk, :]), b32(Ap[:, 1, kk, :])
        TT(k, 'dve', T1c, CTf[:, 0], akr, ALU.mult, R, W)
        TT(k, 'dve', T2c, CTf[:, 1], aki, ALU.mult, R, W)
        TT(k, 'dve', CA[:, 0], T1c, T2c, ALU.subtract, R, W)
        TT(k, 'dve', T1c, CTf[:, 0], aki, ALU.mult, R, W)
        TT(k, 'dve', T2c, CTf[:, 1], akr, ALU.mult, R, W)
        TT(k, 'dve', T1c, T1c, T2c, ALU.add, R, W)
        TS(k, 'dve', CA[:, 1], T1c, -1.0, None, ALU.mult, None, R, W)
        if kk >= 1:
            for ri in range(2):
                CP(k, 'dve', Lc.WC[:, kk - 1, :, ri, :], CA[:, ri], R, [Lc.WC])
        if kk <= 7:
            CP(k, 'dve', CAb, CA, R, W)
            for ti in range(3):
                ps = nextps(k)
                for q in range(TPAIRS[ti]):
                    pr = 3 * ti + q
                    for ri in range(2):
                        MM(k, ps[32 * q:32 * q + 32, 0:32], Lmb[:, ri, pr, :], CAb[:, ri, pr, :], ri == 0, ri == 1, [A], [ps])
                for q in range(TPAIRS[ti]):
                    CP(k, 'act', Lc.Wt[32 * q:32 * q + 32, ti, kk, 32 * q:32 * q + 32], ps[32 * q:32 * q + 32, 0:32], [ps], [Lc.Wt])
XPAIRS = [0, 1]


def slab_iter_init(k, n_tiles):
    k.slab_plan = [(l, j) for _ in range(n_tiles) for l in k.layers for j in range(NSLAB_L)]
    k.slab_issued = 0
    k.slab_used = 0
    k.slab_bufs = {}


def get_slab(k):
    while k.slab_issued < len(k.slab_plan) and k.slab_issued < k.slab_used + NSLOT:
        l, j = k.slab_plan[k.slab_issued]
        k.slab_bufs[k.slab_issued] = load_slab(k, l, j)
        k.slab_issued += 1
    b = k.slab_bufs.pop(k.slab_used)
    k.slab_used += 1
    return b


def evac(k, out, in_, R, W, func=None, eng=None):
    if func is not None:
        AF(k, out, in_, func, R, W)
        return
    e = eng or ('act' if getattr(k, 'force_act', False) else evac_engine(k))
    CP(k, e, out, in_, R, W)


def ln_feed(k, oc):
    x1, x1p = BV(k, 0, 8)
    sq, sqp = BV(k, 8, 4, BF16)
    x13 = x1.rearrange('p (a b) -> p a b', a=8)
    sq3 = sq.rearrange('p (a b) -> p a b', a=8)
    CP(k, 'act', k.hb[:, oc, :], x13[:, oc, :], [(k.B, oc)], [(k.hb, oc)])
    AF(k, sq3[:, oc, :], x13[:, oc, :], ACT.Square, [(k.B, oc)], [(k.B, 8 + oc // 2)])


def ln_begin(k):
    pass


def ln_stats_mm(k, upto):
    pass


def layer_norm_fm(k, l, which):
    x1, x1p = BV(k, 0, 8)
    sq, sqp = BV(k, 8, 4, BF16)
    sq3 = sq.rearrange('p (a b) -> p a b', a=8)
    h, hb = k.h, k.hb
    x13 = x1.rearrange('p (a b) -> p a b', a=8)
    psM, psQ = nextps(k), nextps(k)
    for kc in range(8):
        MM(k, psM[:], k.onesD[:], hb[:, kc, :], kc == 0, kc == 7, [(hb, kc), k.onesD], [psM])
    for kc in range(8):
        MM(k, psQ[:], k.onesD[:], sq3[:, kc, :], kc == 0, kc == 7, [(k.B, 8 + kc // 2), k.onesD], [psQ])
    t, tp = AV(k, 16, 4)
    msq, var, rstd, mean = t[:, 0:512], t[:, 512:1024], t[:, 1024:1536], t[:, 1536:2048]
    AF(k, msq, psM[:], ACT.Square, [psM], [(k.A, 16)])
    CP(k, 'act', mean, psM[:], [psM], [(k.A, 19)])
    TT(k, 'dve', var, psQ[:], msq, ALU.subtract, [psQ, (k.A, 16)], [(k.A, 17)])
    AF(k, var, var, ACT.Ln, [(k.A, 17)], [(k.A, 17)], bias=LN_EPS)
    AF(k, rstd, var, ACT.Exp, [(k.A, 17)], [(k.A, 18)], scale=-0.5)
    gi = 2 + 4 * l + 2 * which
    for kc in range(8):
        if kc in (2, 5, 7):
            TT(k, 'pool', x13[:, kc, :], x13[:, kc, :], mean, ALU.subtract, [(k.B, kc), (k.A, 19)], [(k.B, kc)])
            TT(k, 'pool', x13[:, kc, :], x13[:, kc, :], rstd, ALU.mult, [(k.B, kc), (k.A, 18)], [(k.B, kc)])
        else:
            TT(k, 'dve', x13[:, kc, :], x13[:, kc, :], psM[:], ALU.subtract, [(k.B, kc), psM], [(k.B, kc)])
            TT(k, 'dve', x13[:, kc, :], x13[:, kc, :], rstd, ALU.mult, [(k.B, kc), (k.A, 18)], [(k.B, kc)])
    for kc in range(8):
        AF(k, hb[:, kc, :], x13[:, kc, :], ACT.Identity, [(k.B, kc), k.lnp], [(hb, kc)],
           scale=k.lnp[:, gi, kc:kc + 1], bias=k.lnp[:, gi + 1, kc:kc + 1])
    for kc in range(8):
        if kc % 2 == 0:
            AF(k, h[:, kc, :], x13[:, kc, :], ACT.Identity, [(k.B, kc), k.lnp], [(h, kc)],
               scale=k.lnp[:, gi, kc:kc + 1], bias=k.lnp[:, gi + 1, kc:kc + 1])
        else:
            TS(k, 'pool', h[:, kc, :], x13[:, kc, :], k.lnp[:, gi, kc:kc + 1], k.lnp[:, gi + 1, kc:kc + 1], ALU.mult, ALU.add,
               [(k.B, kc), k.lnp], [(h, kc)])


def tile_layer(k, i, l):
    s = k.s
    Lc = k.L[l]
    h, hb = k.h, k.hb
    sm = Lc.sm
    dbg = k.dbg if (k.dbg and k.dbg[1] == (i, l)) else None
    u_f, u_fp = BV(k, 0, 3)
    ub_, u_bp = BV(k, 3, 2, BF16)
    u_b = ub_[:, 0:1536]
    gz_b, gz_bp = ub_[:, 1536:2048], u_bp
    q_b, q_bp = BV(k, 5, 1, BF16)
    k_b, k_bp = BV(k, 6, 1, BF16)
    vtok, vtokp = BV(k, 7, 2, BF16)
    r_s, r_sp = BV(k, 9, 2, BF16)
    lg, lgp = BV(k, 11, 1, BF16)
    u_f3 = u_f.rearrange('p (a b) -> p a b', a=3)
    u_b3 = u_b.rearrange('p (a b) -> p a b', a=3)
    q_b3 = q_b.rearrange('p (a b) -> p a b', a=2)
    k_b3 = k_b.rearrange('p (a b) -> p a b', a=2)
    vtok3 = vtok.rearrange('p (a b) -> p a b', a=4)
    r_s3 = r_s.rearrange('p (a b) -> p a b', a=4)
    lg3 = lg.rearrange('p (a b) -> p a b', a=2)
    lx = k.lx
    mix, _ = AV(k, 20, 4, BF16)
    mix3 = mix.rearrange('p (a b) -> p a b', a=8)
    mixp = lambda kc: (k.A, 20 + kc // 2)

    def proj(slab3, c0, evf):
        ps = nextps(k)
        for kc in range(8):
            MM(k, ps[:], slab3[:, kc, c0:c0 + 128], hb[:, kc, :], kc == 0, kc == 7, [slabB, (hb, kc)], [ps])
        evf(ps)

    slabB = get_slab(k)
    sl = slabB[:].rearrange('p (kc n) -> p kc n', kc=8)
    for ti in range(3):
        rw = TROWS[ti]
        ps = nextps(k)
        for kc in range(8):
            MM(k, ps[0:rw, :], sl[:, kc, 96 * ti:96 * ti + rw], hb[:, kc, :], kc == 0, kc == 7, [slabB, (hb, kc)], [ps])
        CP(k, 'act', u_f3[0:rw, ti, :], ps[0:rw, :], [ps], [u_fp])
        CP(k, 'dve', u_b3[0:rw, ti, :], ps[0:rw, :], [ps], [u_bp])
    for oc in range(2):
        proj(sl, 256 + oc * 128, lambda ps, oc=oc: evac(k, q_b3[:, oc, :], ps[:], [ps], [q_bp]))
    def a3(ap, a):
        return ap.rearrange('p (a b) -> p a b', a=a)
    psX = [[nextps(k) for q in range(3)] for ri in range(2)]
    for ri in range(2):
        for pr in range(8):
            ti, q = pr // 3, pr % 3
            for j in range(8):
                rhs = u_b3[32 * q:32 * q + 32, ti, :].rearrange('p (i j) -> p j i', j=8)[:, j, :]
                MM(k, psX[ri][q][:, ti * 64:(ti + 1) * 64], Lc.Wx[32 * q:32 * q + 32, ti, j, ri, :], rhs, j == 0, j == 7,
                   [Lc.Wx, u_bp], [psX[ri][q]])
    E = Lc.E
    Ec, Es = E[:, 0].rearrange('p a b -> p (a b)'), E[:, 1].rearrange('p a b -> p (a b)')
    tP, tPp = AV(k, 0, 6)
    Pr, Pi, Gr, Gi, t1, t2 = [tP[:, c * 512:(c + 1) * 512] for c in range(6)]
    RW = [tPp]
    def xmul(dst, ri, tab):
        d3 = a3(dst, 8)
        for q in range(3):
            npq = 3 if q < 2 else 2
            TT(k, 'dve', d3[:, q::3, :], psX[ri][q][:, 0:npq * 64].rearrange('p (a b) -> p a b', b=64), E[:, tab, q::3, :], ALU.mult,
               [psX[ri][q], E], RW)
    xmul(t1, 0, 0)
    xmul(t2, 1, 1)
    TT(k, 'pool', Pr, t1, t2, ALU.add, RW, RW)
    xmul(Gr, 1, 0)
    xmul(Gi, 0, 1)
    TT(k, 'pool', Pi, Gr, Gi, ALU.subtract, RW, RW)
    Sf = Lc.Sf
    for ri, Pp in enumerate([Pr, Pi]):
        P3 = a3(Pp, 8)
        TT(k, 'dve', t1[:, 0:8], Sf[:, ri, :, 0], Lc.rho1[:], ALU.mult, [Sf, Lc.rho1] + RW, RW)
        TT(k, 'dve', P3[:, :, 0], P3[:, :, 0], t1[:, 0:8], ALU.add, RW, RW)
    rho = Lc.rho[:].rearrange('p a b -> p (a b)')
    for Pp, Gp in [(Pr, Gr), (Pi, Gi)]:
        s.add('dve', lambda e, Pp=Pp, Gp=Gp: e.tensor_tensor_scan(out=Gp, data0=rho, data1=Pp, initial=0.0,
                                                                  op0=ALU.mult, op1=ALU.add), [Lc.rho] + RW, RW)
    Sr3, Si3 = Sf[:, 0, :, 1:65], Sf[:, 1, :, 1:65]
    TT(k, 'dve', t1, Gr, Ec, ALU.mult, RW + [E], RW)
    TT(k, 'pool', t2, Gi, Es, ALU.mult, RW + [E], RW)
    TT(k, 'dve', Sr3, a3(t1, 8), a3(t2, 8), ALU.subtract, RW, [Sf])
    TT(k, 'dve', t1, Gr, Es, ALU.mult, RW + [E], RW)
    TT(k, 'pool', t2, Gi, Ec, ALU.mult, RW + [E], RW)
    TT(k, 'dve', Si3, a3(t1, 8), a3(t2, 8), ALU.add, RW, [Sf])
    Sb, Sbp = AV(k, 19, 1, BF16)
    Sb4 = Sb.rearrange('p (r a b) -> p r a b', r=2, a=8)
    CP(k, 'act', Sb4, Sf[:, :, :, 0:64], [Sf], [Sbp])
    CP(k, 'dve', Sf[:, :, :, 0], Sf[:, :, :, 64], [Sf], [Sf])
    k.force_act = True
    slabB = get_slab(k)
    sl = slabB[:].rearrange('p (kc n) -> p kc n', kc=8)
    for oc in range(2):
        proj(sl, oc * 128, lambda ps, oc=oc: evac(k, k_b3[:, oc, :], ps[:], [ps], [k_bp]))
    ps = nextps(k)
    for kc in range(8):
        MM(k, ps[0:16, :], sl[:, kc, 256:272], hb[:, kc, :], kc == 0, kc == 7, [slabB, (hb, kc)], [ps])
    evac(k, gz_b[0:16, 0:512], ps[0:16, :], [ps], [gz_bp])
    slabB = get_slab(k)
    sl = slabB[:].rearrange('p (kc n) -> p kc n', kc=8)
    for tb in range(4):
        ps = nextps(k)
        for kc in range(8):
            MM(k, ps[:], hb[:, kc, tb * 128:(tb + 1) * 128], sl[:, kc, :], kc == 0, kc == 7, [slabB, (hb, kc)], [ps])
        evac(k, vtok3[:, tb, :], ps[:], [ps], [vtokp])
    slabB = get_slab(k)
    sl = slabB[:].rearrange('p (kc n) -> p kc n', kc=8)
    for oc in range(4):
        proj(sl, oc * 128, lambda ps, oc=oc: evac(k, r_s3[:, oc, :], ps[:], [ps], [r_sp], func=ACT.Silu))
    slabB = get_slab(k)
    sl = slabB[:].rearrange('p (kc n) -> p kc n', kc=8)
    for ct in range(2):
        CP(k, 'pool', lx[:, ct, 0:3], Lc.lxh[:, ct, :], [Lc.lxh], [(lx, ct)])
        proj(sl, ct * 128, lambda ps, ct=ct: evac(k, lx[:, ct, 3:515], ps[:], [ps], [(lx, ct)]))
    for ct in range(2):
        proj(sl, 256 + ct * 128, lambda ps, ct=ct: evac(k, lg3[:, ct, :], ps[:], [ps], [lgp], func=ACT.Gelu_apprx_tanh))
    k.force_act = False
    if dbg and ('z' in dbg[0].split(',')):
        dump(k, 'u', u_f, (128, 1536), [u_fp])
        dump(k, 'q', q_b, (128, 1024), [q_bp], BF16)
        dump(k, 'v', vtok, (128, 2048), [vtokp], BF16)
        dump(k, 'lx', lx[:], (128, 2, 515), [lx])
        dump(k, 'gz', gz_b[0:16, 0:512], (16, 512), [gz_bp], BF16)

    gl, glp = AV(k, 0, 2)
    bc, bcp = AV(k, 2, 2)
    ee, eep = AV(k, 4, 2)
    dl, dlp = AV(k, 6, 2)
    qem, qemp = AV(k, 8, 2, BF16)
    ke, kep = AV(k, 10, 1, BF16)
    kd, kdp = AV(k, 11, 1, BF16)
    kdT, kdTp = AV(k, 12, 1, BF16)
    sT, sTp = AV(k, 13, 2, BF16)
    Sgb, Sgbp = AV(k, 15, 2, BF16)
    dec, decp = Lc.dec[:], Lc.dec
    osq, osqp = AV(k, 17, 2, BF16)
    gl3, bc3, ee3, dl3 = a3(gl, 2), a3(bc, 2), a3(ee, 2), a3(dl, 2)
    qem4 = qem.rearrange('p (b c t) -> p b c t', b=2, c=2)
    ke3, kd3 = a3(ke, 2), a3(kd, 2)
    kdT4 = kdT.rearrange('p (c t d) -> p c t d', c=2, t=4)
    sT4 = sT.rearrange('p (h n i) -> p h n i', h=4, n=8)
    Sgb4 = Sgb.rearrange('p (n c e) -> p n c e', n=8, c=2)
    dec3 = dec[:, 0:16].rearrange('p (c n) -> p c n', c=2)
    osq3 = a3(osq, 4)
    MS(k, 'pool', qem4[64:128, 0], 0.0, [qemp])
    MS(k, 'pool', qem4[0:64, 1], 0.0, [qemp])
    for ct in range(2):
        ps = nextps(k)
        MM(k, ps[:], Lc.wup[0:16, ct * 128:(ct + 1) * 128], gz_b[0:16, 0:512], True, True, [Lc.wup, gz_bp], [ps])
        AF(k, gl3[:, ct, :], ps[:], ACT.Exp, [ps, sm], [glp], scale=-1.0, bias=sm[:, 2 + ct:3 + ct])
    AF(k, gl, gl, ACT.Ln, [glp], [glp], bias=1.0)
    for ct in range(2):
        s.add('dve', lambda e, ct=ct: e.tensor_tensor_scan(out=bc3[:, ct, :], data0=k.cmask[:], data1=gl3[:, ct, :], initial=0.0,
                                                           op0=ALU.mult, op1=ALU.add), [k.cmask, glp], [bcp])
    bc4 = bc.rearrange('p (c n i) -> p c n i', c=2, n=8)
    dl4 = dl.rearrange('p (c n i) -> p c n i', c=2, n=8)
    AF(k, ee, bc, ACT.Exp, [bcp], [eep], scale=-1.0 / 16)
    AF(k, dec3, bc4[:, :, :, 63], ACT.Exp, [bcp], [decp], scale=-1.0 / 16)
    TT(k, 'dve', dl4, bc4[:, :, :, 63:64].broadcast_to([128, 2, 8, 64]), bc4, ALU.subtract, [bcp], [dlp])
    q_b3v = q_b.rearrange('p (a b) -> p a b', a=2)
    for hb_ in range(2):
        rws = slice(hb_ * 64, hb_ * 64 + 64)
        STT(k, qem4[rws, hb_], q_b3v[rws], 0.125, ee3[rws], ALU.mult, ALU.mult, [q_bp, eep], [qemp])
    AF(k, dl, dl, ACT.Exp, [dlp], [dlp], scale=-1.0 / 16)
    AF(k, ee, bc, ACT.Exp, [bcp, qemp], [eep], scale=1.0 / 16)
    TT(k, 'dve', kd, k_b, dl, ALU.mult, [k_bp, dlp], [kdp])
    TT(k, 'dve', ke, k_b, ee, ALU.mult, [k_bp, eep], [kep])
    y_f, y_fp = AV(k, 12, 3)
    y_f3 = a3(y_f, 3)
    yg_b, yg_bp = AV(k, 15, 2, BF16)
    yg_b3 = yg_b[:, 0:1536].rearrange('p (a b) -> p a b', a=3)
    for ti in range(3):
        rw = TROWS[ti]
        psY = nextps(k)
        psY3 = psY[:].rearrange('p (i t) -> p i t', t=8)
        u3 = u_b3[:, ti, :].rearrange('p (i t) -> p i t', t=8)
        for lag in range(8):
            MM(k, psY3[0:rw, :, lag:8], Lc.Wt[0:rw, ti, lag, 0:rw], u3[0:rw, :, 0:8 - lag], lag == 0, False, [Lc.Wt, u_bp], [psY])
        for t in range(8):
            for q in range(TPAIRS[ti]):
                pr = 3 * ti + q
                for ri in range(2):
                    last = (t == 7 and q == TPAIRS[ti] - 1 and ri == 1)
                    MM(k, psY3[32 * q:32 * q + 32, :, t], Lc.WC[:, t, pr, ri, :], Sb4[:, ri, pr, :], False, last, [Lc.WC, Sbp], [psY])
        STT(k, y_f3[0:rw, ti, :], u_f3[0:rw, ti, :], Lc.sd[0:rw, ti:ti + 1], psY[0:rw, :], ALU.mult, ALU.add, [u_fp, Lc.sd, psY], [y_fp])
        if rw == 96:
            MS(k, 'pool', yg_b3[64:128, ti, :], 0.0, [yg_bp])
        else:
            MS(k, 'pool', yg_b3[64:128, ti, :], 0.0, [yg_bp])
        AF(k, yg_b3[0:rw, ti, :], y_f3[0:rw, ti, :], ACT.Gelu_apprx_tanh, [y_fp], [yg_bp])
    ys_f, ys_fp = AV(k, 17, 2)
    ys_f3 = a3(ys_f, 2)
    ys_b, ys_bp = BV(k, 0, 1, BF16)
    ys_b3 = a3(ys_b, 2)
    sI = k.shiftI
    conv = [[(0, 0, 128, 0), (1, 0, 128, 96)], [(1, 0, 128, -32), (2, 0, 128, 64)]]
    for oc in range(2):
        ps = nextps(k)
        for ci, (ti, r0, nr, sh) in enumerate(conv[oc]):
            MM(k, ps[:], sI[r0:r0 + nr, 128 - sh:256 - sh], yg_b3[r0:r0 + nr, ti, :], ci == 0, ci == len(conv[oc]) - 1, [sI, yg_bp], [ps])
        CP(k, 'act', ys_f3[:, oc, :], ps[:], [ps], [ys_fp])
        CP(k, 'dve', ys_b3[:, oc, :], ps[:], [ps], [ys_bp])
    sg, sgp = BV(k, 1, 1)
    for oc in range(2):
        ps = nextps(k)
        for kc in range(2):
            MM(k, ps[:], Lc.wglu[:, kc, oc * 128:(oc + 1) * 128], ys_b3[:, kc, :], kc == 0, kc == 1, [Lc.wglu, ys_bp], [ps])
        AF(k, sg, ps[:], ACT.Sigmoid, [ps], [sgp])
        TT(k, 'dve', mix3[:, oc, :], ys_f3[:, oc, :], sg, ALU.mult, [ys_fp, sgp], [mixp(oc)])
    if dbg and ('s5' in dbg[0].split(',')):
        dump(k, 'y', y_f, (128, 1536), [y_fp])
        dump(k, 'Sf', Sf[:], (128, 2, 8, 65), [Sf])
        dump(k, 'E', E[:], (128, 2, 8, 64), [E])
        dump(k, 'mixs5', mix[:, 0:1024], (128, 1024), [mixp(0)], BF16)

    if k.stop == 'stop_s5':
        return
    sT5 = sT.rearrange('p (h m two i) -> p h m two i', h=4, m=4, two=2)
    MS(k, 'pool', sT5[0:64, :, :, 1, :], 0.0, [sTp])
    MS(k, 'pool', sT5[64:128, :, :, 0, :], 0.0, [sTp])
    psT = nextps(k)
    psTb = psT[:].bitcast(BF16)
    for ct in range(2):
        for tb in range(4):
            o = psTb[:, (ct * 4 + tb) * 128:(ct * 4 + tb + 1) * 128]
            s.add('pe', lambda e, o=o, ct=ct, tb=tb: e.transpose(out=o, in_=kd3[:, ct, tb * 128:(tb + 1) * 128], identity=k.identb[:]),
                  [kdp, k.identb], [psT])
    CP(k, 'act', kdT, psTb, [psT], [kdTp])
    psS = [nextps(k), nextps(k)]
    for hh in range(4):
        ct, hb_ = hh // 2, hh % 2
        for n in range(8):
            o = psS[hb_][(n % 2) * 64:(n % 2) * 64 + 64, ct * 256 + (n // 2) * 64: ct * 256 + (n // 2) * 64 + 64]
            MM(k, o, ke3[hb_ * 64:hb_ * 64 + 64, ct, n * 64:(n + 1) * 64], qem4[hb_ * 64:hb_ * 64 + 64, hb_, ct, n * 64:(n + 1) * 64],
               True, True, [kep, qemp], [psS[hb_]])
    sT6 = sT.rearrange('p (c b m two i) -> p c b m two i', c=2, b=2, m=4, two=2)
    for hb_ in range(2):
        for par in range(2):
            rws = slice(par * 64, par * 64 + 64)
            TT(k, 'dve', sT6[rws, :, hb_, :, par, :], psS[hb_][rws, :].rearrange('p (c m i) -> p c m i', c=2, m=4),
               k.tri[rws, :].unsqueeze(1).unsqueeze(1).broadcast_to([64, 2, 4, 64]), ALU.mult, [psS[hb_], k.tri], [sTp])
    Sg = Lc.Sg
    for n in range(8):
        psU = nextps(k)
        for ct in range(2):
            for hb_ in range(2):
                hh = 2 * ct + hb_
                MM(k, psU[hb_ * 64:hb_ * 64 + 64, ct * 128:(ct + 1) * 128],
                   kdT4[(n % 2) * 64:(n % 2) * 64 + 64, ct, n // 2, hb_ * 64:hb_ * 64 + 64],
                   vtok3[(n % 2) * 64:(n % 2) * 64 + 64, n // 2, hh * 128:(hh + 1) * 128], True, True, [kdTp, vtokp], [psU])
        CP(k, 'act', Sgb4[:, n], Sg[:], [Sg], [Sgbp])
        for ct in range(2):
            STT(k, Sg[:, ct, :], Sg[:, ct, :], dec3[:, ct, n:n + 1], psU[:, ct * 128:(ct + 1) * 128], ALU.mult, ALU.add,
                [Sg, decp, psU], [Sg])
    psO = [nextps(k) for _ in range(4)]
    for hh in range(4):
        ct, hb_ = hh // 2, hh % 2
        for n in range(8):
            o = psO[hh][:, n * 64:(n + 1) * 64]
            MM(k, o, Sgb4[:, n, ct, :], qem4[:, hb_, ct, n * 64:(n + 1) * 64], True, False, [Sgbp, qemp], [psO[hh]])
            MM(k, o, vtok3[:, n // 2, hh * 128:(hh + 1) * 128], sT4[:, hh, n, :], False, True, [vtokp, sTp], [psO[hh]])
    for hh in range(4):
        AF(k, osq3[:, hh, :], psO[hh][:], ACT.Square, [psO[hh]], [(k.A, 17 + hh // 2)])
    psR = [nextps(k) for _ in range(4)]
    for hh in range(4):
        MM(k, psR[hh][:], k.onesV[:], osq3[:, hh, :], True, True, [k.onesV, (k.A, 17 + hh // 2)], [psR[hh]])
    rsb, _ = AV(k, 0, 4)
    onb, _ = AV(k, 4, 4)
    rs3, on3 = a3(rsb, 4), a3(onb, 4)
    for hh in range(4):
        AF(k, rs3[:, hh, :], psR[hh][:], ACT.Ln, [psR[hh]], [(k.A, hh)], bias=RMS_EPS)
        AF(k, rs3[:, hh, :], rs3[:, hh, :], ACT.Exp, [(k.A, hh)], [(k.A, hh)], scale=-0.5)
    for hh in range(4):
        TT(k, 'dve', on3[:, hh, :], psO[hh][:], rs3[:, hh, :], ALU.mult, [psO[hh], (k.A, hh)], [(k.A, 4 + hh)])
        STT(k, mix3[:, 2 + hh, :], on3[:, hh, :], sm[:, 4:5], r_s3[:, hh, :], ALU.mult, ALU.mult, [(k.A, 4 + hh), sm, r_sp], [mixp(2 + hh)])
    if dbg and ('gla' in dbg[0].split(',')):
        dump(k, 'bc', bc, (128, 1024), [bcp])
        dump(k, 'Sg', Sg[:], (128, 2, 128), [Sg])
        dump(k, 'mixgla', mix[:, 1024:3072], (128, 2048), [(k.A, [21, 22])], BF16)

    if k.stop == 'stop_gla':
        return
    xc, xcp = AV(k, 0, 2)
    xcb, xcbp = AV(k, 2, 1, BF16)
    gr, grp = AV(k, 3, 2)
    gi_, gip = AV(k, 5, 2)
    aa, aap = AV(k, 7, 2)
    mm_, mmp = AV(k, 9, 2)
    hs, hsp = AV(k, 11, 2)
    xc3, xcb3, gr3, gi3, aa3, mm3, hs3 = a3(xc, 2), a3(xcb, 2), a3(gr, 2), a3(gi_, 2), a3(aa, 2), a3(mm_, 2), a3(hs, 2)
    for ct in range(2):
        TS(k, 'dve', xc3[:, ct, :], lx[:, ct, 3:515], sm[:, 5 + 4 * ct + 3:5 + 4 * ct + 4], sm[:, 13 + ct:14 + ct], ALU.mult, ALU.add,
           [(lx, ct), sm], [xcp])
        for kk in range(3):
            STT(k, xc3[:, ct, :], lx[:, ct, kk:kk + 512], sm[:, 5 + 4 * ct + kk:5 + 4 * ct + kk + 1], xc3[:, ct, :], ALU.mult, ALU.add,
                [(lx, ct), sm, xcp], [xcp])
        CP(k, 'pool', Lc.lxh[:, ct, :], lx[:, ct, 512:515], [(lx, ct)], [Lc.lxh])
    CP(k, 'act', xcb, xc, [xcp], [xcbp])
    for ct in range(2):
        for g_, (dst, dstp, bcol) in enumerate([(gr3, grp, 15), (gi3, gip, 17)]):
            ps = nextps(k)
            MM(k, ps[:], Lc.lgw[:, ct, g_, :], xcb3[:, ct, :], True, True, [Lc.lgw, xcbp], [ps])
            AF(k, dst[:, ct, :], ps[:], ACT.Sigmoid, [ps, sm], [dstp], bias=sm[:, bcol + ct:bcol + ct + 1])
    for ct in range(2):
        AF(k, aa3[:, ct, :], gr3[:, ct, :], ACT.Exp, [grp, sm], [aap], scale=sm[:, 19 + ct:20 + ct])
    AF(k, mm_, aa, ACT.Square, [aap], [mmp])
    AF(k, mm_, mm_, ACT.Ln, [mmp], [mmp], scale=-1.0, bias=1.0)
    AF(k, mm_, mm_, ACT.Exp, [mmp], [mmp], scale=0.5)
    TT(k, 'dve', mm_, mm_, gi_, ALU.mult, [mmp, gip], [mmp])
    TT(k, 'dve', mm_, mm_, xc, ALU.mult, [mmp, xcp], [mmp])
    for ct in range(2):
        s.add('dve', lambda e, ct=ct: e.tensor_tensor_scan(out=hs3[:, ct, :], data0=aa3[:, ct, :], data1=mm3[:, ct, :],
                                                           initial=Lc.lh[:, ct:ct + 1], op0=ALU.mult, op1=ALU.add),
              [aap, mmp, Lc.lh], [hsp])
        CP(k, 'dve', Lc.lh[:, ct:ct + 1], hs3[:, ct, 511:512], [hsp], [Lc.lh])
        TT(k, 'pool', mix3[:, 6 + ct, :], hs3[:, ct, :], lg3[:, ct, :], ALU.mult, [hsp, lgp], [mixp(6 + ct)])
    if dbg and ('mix' in dbg[0].split(',')):
        dump(k, 'mix', mix, (128, 4096), [(k.A, [20, 21, 22, 23])], BF16)

    if k.stop == 'stop_lru':
        return
    x1, x1p = BV(k, 0, 8)
    x13 = x1.rearrange('p (a b) -> p a b', a=8)
    ln_begin(k)
    for j in range(2):
        slabB = get_slab(k)
        sl = slabB[:].rearrange('p (kc n) -> p kc n', kc=8)
        for o4 in range(4):
            oc = 4 * j + o4
            ln_stats_mm(k, oc - 2)
            ps = nextps(k)
            for kc in range(8):
                MM(k, ps[:], sl[:, kc, o4 * 128:(o4 + 1) * 128], mix3[:, kc, :], kc == 0, kc == 7, [slabB, mixp(kc)], [ps])
            STT(k, x13[:, oc, :], h[:, oc, :], ALPHA, ps[:], ALU.mult, ALU.add, [(h, oc), ps], [(k.B, oc)])
            ln_feed(k, oc)
    layer_norm_fm(k, l, 0)
    if dbg and ('h1' in dbg[0].split(',')):
        dump(k, 'h1', h[:], (128, 8, 512), [h])
    if k.stop == 'stop_wout':
        return
    hid, _ = AV(k, 0, 16, BF16)
    hid3 = hid.rearrange('p (a b) -> p a b', a=32)
    rl, rlp = AV(k, 16, 2)
    rl3 = a3(rl, 2)
    for j in range(8):
        slabB = get_slab(k)
        sl = slabB[:].rearrange('p (kc n) -> p kc n', kc=8)
        for o4 in range(4):
            oc = 4 * j + o4
            ps = nextps(k)
            for kc in range(8):
                MM(k, ps[:], sl[:, kc, o4 * 128:(o4 + 1) * 128], hb[:, kc, :], kc == 0, kc == 7, [slabB, (hb, kc)], [ps])
            w = oc % 2
            AF(k, rl3[:, w, :], ps[:], ACT.Relu, [ps], [(k.A, 16 + w)])
            TT(k, 'pool' if oc % 2 else 'dve', hid3[:, oc, :], rl3[:, w, :], rl3[:, w, :], ALU.mult, [(k.A, 16 + w)], [(k.A, oc // 2)])
    ln_begin(k)
    for oc in range(8):
        slabB = get_slab(k)
        sl = slabB[:].rearrange('p (kc n) -> p kc n', kc=32)
        ln_stats_mm(k, oc - 1)
        ps = nextps(k)
        for kc in range(32):
            MM(k, ps[:], sl[:, kc, :], hid3[:, kc, :], kc == 0, kc == 31, [slabB, (k.A, kc // 2)], [ps])
        STT(k, x13[:, oc, :], h[:, oc, :], ALPHA, ps[:], ALU.mult, ALU.add, [(h, oc), ps], [(k.B, oc)])
        ln_feed(k, oc)
    layer_norm_fm(k, l, 1)
    if dbg and ('h2' in dbg[0].split(',')):
        dump(k, 'h2', h[:], (128, 8, 512), [h])


def kernel(**inputs):
    from concourse.bass_utils import run_bass_kernel_spmd
    nc, k = build()
    params = {n: np.ascontiguousarray(np.asarray(inputs[n]), dtype=np.float32) for n in PARAM_SHAPES}
    x = np.asarray(inputs['x'])
    in_maps = [dict(params, x=np.ascontiguousarray(x[b], dtype=np.float32)) for b in range(8)]
    res = run_bass_kernel_spmd(nc, in_maps, core_ids=list(range(8)))
    return np.stack([np.asarray(r['out']) for r in res.results], axis=0).astype(np.float32)
```
